# Optimizing a Trainium2 kernel written in Bass

```python
import jax, jax.numpy as jnp
from jax import lax
import numpy as np

D_MODEL = 2048
BATCH = 16
SEQ = 2048
DEPTH = 4

D_MIX = D_MODEL
N_MIXERS = 4
GROUP_W = D_MIX // N_MIXERS
ATT_HEADS = 4
HEAD_DIM = GROUP_W // ATT_HEADS
Q_BLOCK = 128
CONF_K = 31
SC_K = 3
POOL_WINDOWS = (2, 4, 8, 16)
POOL_GROUP = GROUP_W // len(POOL_WINDOWS)
D_FF = 4 * D_MODEL
EPS = 1e-6
N_ATT_COLS = 3 * GROUP_W + ATT_HEADS
N_CONF_COLS = 2 * GROUP_W
N_POOL_COLS = GROUP_W
N_SC_COLS = 3 * GROUP_W
D_IN = N_ATT_COLS + N_CONF_COLS + N_POOL_COLS + N_SC_COLS

kernel_name = "hybrid_parallel_heads_fox_conformer_pool_shortconv"


def rmsnorm(x, g):
    xf = x.astype(jnp.float32)
    y = xf * lax.rsqrt(jnp.mean(xf * xf, axis=-1, keepdims=True) + EPS)
    return (y * g.astype(jnp.float32)).astype(x.dtype)


def layernorm(x, g, b):
    xf = x.astype(jnp.float32)
    mu = jnp.mean(xf, axis=-1, keepdims=True)
    var = jnp.mean(jnp.square(xf - mu), axis=-1, keepdims=True)
    y = (xf - mu) * lax.rsqrt(var + EPS)
    return (y * g.astype(jnp.float32) + b.astype(jnp.float32)).astype(x.dtype)


def causal_depthwise_conv(x, w):
    k, c = w.shape
    return lax.conv_general_dilated(
        x, w[:, None, :].astype(x.dtype), window_strides=(1,), padding=[(k - 1, 0)],
        dimension_numbers=("NWC", "WIO", "NWC"), feature_group_count=c)


def forgetting_attention(q, k, v, log_f):
    s_len = q.shape[1]
    cum = jnp.transpose(jnp.cumsum(log_f, axis=1), (0, 2, 1))
    scale = 1.0 / np.sqrt(HEAD_DIM)
    outs = []
    for i in range(s_len // Q_BLOCK):
        q0, q1 = i * Q_BLOCK, (i + 1) * Q_BLOCK
        qb = q[:, q0:q1]
        kb, vb = k[:, :q1], v[:, :q1]
        scores = jnp.einsum("bqhd,bkhd->bhqk", qb, kb).astype(jnp.float32) * scale
        decay = cum[:, :, q0:q1, None] - cum[:, :, None, :q1]
        q_pos = jnp.arange(q0, q1)[:, None]
        k_pos = jnp.arange(q1)[None, :]
        logits = jnp.where(k_pos <= q_pos, scores + decay, -jnp.inf)
        p = jax.nn.softmax(logits, axis=-1)
        outs.append(jnp.einsum("bhqk,bkhd->bqhd", p.astype(vb.dtype), vb))
    return jnp.concatenate(outs, axis=1)


def multiscale_pool(xp, pool_w, pool_scale):
    b, s, _ = xp.shape
    xg = xp.reshape(b, s, len(POOL_WINDOWS), POOL_GROUP)
    pos = jnp.arange(s)
    pooled = []
    for g, w in enumerate(POOL_WINDOWS):
        xf = xg[:, :, g].astype(jnp.float32)
        cs = jnp.cumsum(xf, axis=1)
        prev = jnp.pad(cs, ((0, 0), (w, 0), (0, 0)))[:, :s]
        count = jnp.minimum(pos + 1, w).astype(jnp.float32)[None, :, None]
        pooled.append((cs - prev) / count - xf)
    pooled = jnp.stack(pooled, axis=2).astype(xp.dtype)
    y = jnp.einsum("bsgc,gcd->bsgd", pooled, pool_w).reshape(b, s, GROUP_W)
    return y * pool_scale


def setup_inputs(seed: int = 0) -> dict:
    key = jax.random.key(seed)
    ks = jax.random.split(key, 20)
    f32 = jnp.float32
    nrm = lambda k, shape, s: jax.random.normal(k, shape, f32) * s
    gain = lambda k, shape: 1.0 + 0.05 * jax.random.normal(k, shape, f32)
    return {
        "x": jax.random.normal(ks[0], (BATCH, SEQ, D_MODEL), f32),
        "mix_norm_pre": gain(ks[1], (DEPTH, D_MODEL)),
        "w_in": nrm(ks[2], (DEPTH, D_MODEL, D_IN), D_MODEL ** -0.5),
        "b_forget": 3.0 + 0.5 * jax.random.normal(ks[3], (DEPTH, ATT_HEADS), f32),
        "conf_dw": nrm(ks[4], (DEPTH, CONF_K, GROUP_W), CONF_K ** -0.5),
        "conf_ln_g": gain(ks[5], (DEPTH, GROUP_W)),
        "conf_ln_b": nrm(ks[6], (DEPTH, GROUP_W), 0.02),
        "pool_w": nrm(ks[7], (DEPTH, len(POOL_WINDOWS), POOL_GROUP, POOL_GROUP), POOL_GROUP ** -0.5),
        "pool_scale": gain(ks[8], (DEPTH, GROUP_W)),
        "sc_dw": nrm(ks[9], (DEPTH, SC_K, GROUP_W), SC_K ** -0.5),
        "w_out": nrm(ks[10], (DEPTH, D_MIX, D_MODEL), D_MIX ** -0.5),
        "mix_norm_post": gain(ks[11], (DEPTH, D_MODEL)),
        "mlp_norm_pre": gain(ks[12], (DEPTH, D_MODEL)),
        "w_mlp1": nrm(ks[13], (DEPTH, D_MODEL, D_FF), D_MODEL ** -0.5),
        "w_mlp2": nrm(ks[14], (DEPTH, D_FF, D_MODEL), D_FF ** -0.5),
        "mlp_norm_post": gain(ks[15], (DEPTH, D_MODEL)),
    }


def reference(x, mix_norm_pre, w_in, b_forget, conf_dw, conf_ln_g, conf_ln_b,
              pool_w, pool_scale, sc_dw, w_out, mix_norm_post, mlp_norm_pre,
              w_mlp1, w_mlp2, mlp_norm_post):
    b, s, _ = x.shape
    o1 = N_ATT_COLS
    o2 = o1 + N_CONF_COLS
    o3 = o2 + N_POOL_COLS
    for l in range(DEPTH):
        h = rmsnorm(x, mix_norm_pre[l])
        proj = jnp.einsum("bsd,de->bse", h, w_in[l])
        att_in, conf_in, pool_in, sc_in = proj[..., :o1], proj[..., o1:o2], proj[..., o2:o3], proj[..., o3:]

        q = att_in[..., :GROUP_W].reshape(b, s, ATT_HEADS, HEAD_DIM)
        k = att_in[..., GROUP_W:2 * GROUP_W].reshape(b, s, ATT_HEADS, HEAD_DIM)
        v = att_in[..., 2 * GROUP_W:3 * GROUP_W].reshape(b, s, ATT_HEADS, HEAD_DIM)
        log_f = jax.nn.log_sigmoid(att_in[..., 3 * GROUP_W:].astype(jnp.float32)
                                   + b_forget[l].astype(jnp.float32))
        y_att = forgetting_attention(q, k, v, log_f).reshape(b, s, GROUP_W)

        a, g = jnp.split(conf_in, 2, axis=-1)
        c = causal_depthwise_conv(a * jax.nn.sigmoid(g), conf_dw[l])
        y_conf = jax.nn.silu(layernorm(c, conf_ln_g[l], conf_ln_b[l]))

        y_pool = multiscale_pool(pool_in, pool_w[l], pool_scale[l])

        bg, cg, hs = jnp.split(sc_in, 3, axis=-1)
        y_sc = bg * causal_depthwise_conv(cg * hs, sc_dw[l])

        y = jnp.concatenate([y_att, y_conf, y_pool, y_sc], axis=-1)
        y = jnp.einsum("bse,ed->bsd", y, w_out[l])
        x = x + rmsnorm(y, mix_norm_post[l])

        h = rmsnorm(x, mlp_norm_pre[l])
        u = jnp.square(jax.nn.relu(jnp.einsum("bsd,df->bsf", h, w_mlp1[l])))
        y = jnp.einsum("bsf,fd->bsd", u, w_mlp2[l])
        x = x + rmsnorm(y, mlp_norm_post[l])
    return x
```

```python
import numpy as np
from contextlib import ExitStack
import concourse.bass as bass
import concourse.mybir as mybir
from concourse.bass_utils import run_bass_kernel_spmd

F32 = mybir.dt.float32
BF16 = mybir.dt.bfloat16
AF = mybir.ActivationFunctionType
ALU = mybir.AluOpType

D = 2048
DC = 16
T = 512
DFF = 8192
EPS = 1e-6
NSLOT = 90
SLOT = 4096
NR = 3
SCALE = 1.0 / np.sqrt(128.0)

O_G1, O_G2, O_G3, O_G4 = 0, 16, 32, 48
O_CDW = 64
O_LNG = O_CDW + 124
O_LNB = O_LNG + 4
O_PSC = O_LNB + 4
O_SDW = O_PSC + 4
O_BF = O_SDW + 12
O_WF = O_BF + 16
O_PW = O_WF + 64
NSP = O_PW + 512
O_TRI, O_ONE, O_INV = 0, 128, 256
NCST = 320

def _col(kind, i):
    base = {"q": 0, "k": 512, "a": 1540, "g": 2052, "p": 2564, "B": 3076, "C": 3588, "H": 4100}[kind]
    return base + 128 * i

WIN_ORDER = ([("k", i) for i in range(4)] + ["V"] + [("q", i) for i in range(4)]
             + [x for c in range(4) for x in (("g", c), ("a", c))]
             + [("p", i) for i in range(4)]
             + [x for c in range(4) for x in (("C", c), ("H", c), ("B", c))])


def _win_slots():
    slots = []
    pend = []
    for it in WIN_ORDER:
        if it == "V":
            assert not pend
            slots.append(("V", 0))
            slots.append(("V", 1))
        else:
            pend.append(it)
            if len(pend) == 2:
                slots.append(("W", pend))
                pend = []
    assert not pend and len(slots) == 18
    return slots


WIN_SLOTS = _win_slots()


def pack_weights_layer(w_in, w_out, w1, w2):
    out = np.empty((NSLOT, 128, SLOT), np.float32)
    s = 0
    wi = w_in.reshape(DC, 128, -1)
    for kind, it in WIN_SLOTS:
        if kind == "V":
            blk = wi[it * 8:(it + 1) * 8, :, 1024:1536]
            out[s] = blk.transpose(1, 0, 2).reshape(128, SLOT)
        else:
            cols = np.concatenate([np.arange(_col(k, i), _col(k, i) + 128) for k, i in it])
            out[s] = wi[:, :, cols].transpose(1, 0, 2).reshape(128, SLOT)
        s += 1
    wo = w_out.reshape(DC, 128, D)
    for g in range(8):
        out[s] = wo[:, :, 256 * g:256 * g + 256].transpose(1, 0, 2).reshape(128, SLOT)
        s += 1
    w1r = w1.reshape(DC, 128, DFF)
    w2r = w2.reshape(64, 128, D)
    for hf in range(2):
        for g in range(16):
            gg = hf * 16 + g
            out[s] = w1r[:, :, 256 * gg:256 * gg + 256].transpose(1, 0, 2).reshape(128, SLOT)
            s += 1
        for dch in range(16):
            out[s] = w2r[hf * 32:(hf + 1) * 32, :, dch * 128:(dch + 1) * 128].transpose(1, 0, 2).reshape(128, SLOT)
            s += 1
    assert s == NSLOT
    return out


def pack_small_layer(l, p):
    sp = np.zeros((128, NSP), np.float32)
    for o, name in ((O_G1, "mix_norm_pre"), (O_G2, "mix_norm_post"), (O_G3, "mlp_norm_pre"), (O_G4, "mlp_norm_post")):
        sp[:, o:o + 16] = p[name][l].reshape(16, 128).T
    sp[:, O_CDW:O_CDW + 124] = p["conf_dw"][l].reshape(31, 4, 128).transpose(2, 1, 0).reshape(128, 124)
    sp[:, O_LNG:O_LNG + 4] = p["conf_ln_g"][l].reshape(4, 128).T
    sp[:, O_LNB:O_LNB + 4] = p["conf_ln_b"][l].reshape(4, 128).T
    sp[:, O_PSC:O_PSC + 4] = p["pool_scale"][l].reshape(4, 128).T
    sp[:, O_SDW:O_SDW + 12] = p["sc_dw"][l].reshape(3, 4, 128).transpose(2, 1, 0).reshape(128, 12)
    sp[:, O_BF:O_BF + 16] = np.tile(p["b_forget"][l][None, :], (128, 4))
    sp[:, O_WF:O_WF + 64] = p["w_in"][l][:, 1536:1540].reshape(16, 128, 4).transpose(1, 0, 2).reshape(128, 64)
    sp[:, O_PW:O_PW + 512] = p["pool_w"][l].transpose(1, 0, 2).reshape(128, 512)
    return sp


def make_consts():
    c = np.zeros((128, NCST + 128), np.float32)
    c[:, NCST:NCST + 128] = np.eye(128, dtype=np.float32)
    s = np.arange(128)
    c[:, O_TRI:O_TRI + 128] = (s[:, None] <= s[None, :]).astype(np.float32)
    c[:, O_ONE:O_ONE + 128] = 1.0
    for g, w in enumerate((2, 4, 8, 16)):
        c[:, O_INV + 16 * g:O_INV + 16 * g + 16] = 1.0 / np.minimum(np.arange(16) + 1, w)
    return c


class Buf:
    __slots__ = ("name", "w", "r", "region", "lo", "hi", "ov")

    def __init__(self, name, region=None, lo=0, hi=0):
        self.name = name
        self.w = None
        self.r = {}
        self.region = region
        self.lo, self.hi = lo, hi
        self.ov = [self]


class Eng:
    def __init__(self, name, key):
        self.name = name
        self.key = key
        self.count = 0
        self.ninstr = 0
        self.last_ev_instr = {}
        self.seen = {}
        self.prog = []
        self.pending = False


class Prog:
    def __init__(self):
        self.engs = {}
        self.regions = {}
        self.dma_val = {}

    def add_engine(self, name, key):
        self.engs[name] = Eng(name, key)

    def buf(self, name, region=None, lo=0, hi=0):
        b = Buf(name, region, lo, hi)
        if region is not None:
            lst = self.regions.setdefault(region, [])
            for o in lst:
                if o.lo < hi and lo < o.hi:
                    o.ov.append(b)
                    b.ov.append(o)
            lst.append(b)
        return b

    def _deps(self, reads, writes):
        deps = {}

        def add(ev):
            if ev is not None:
                k, v = ev
                if deps.get(k, 0) < v:
                    deps[k] = v
        for b0 in reads:
            for b in b0.ov:
                add(b.w)
        for b0 in writes:
            for b in b0.ov:
                add(b.w)
                for k, v in b.r.items():
                    add((k, v))
        return deps

    def _waits(self, E, deps):
        waits = []
        for k, v in deps.items():
            if k == E.key:
                if E.name == "pe":
                    continue
                if E.ninstr - E.last_ev_instr.get(v, -10) > 2:
                    continue
            if E.seen.get(k, 0) >= v:
                continue
            if k.startswith("e_"):
                assert v <= self.engs[k[2:]].count, ("wait on future event", E.name, k, v)
            E.seen[k] = v
            waits.append((k, v))
        return waits

    def op(self, eng, meth, kw, reads=(), writes=(), ev=True):
        fn = (meth, kw)
        E = self.engs[eng]
        waits = self._waits(E, self._deps(reads, writes))
        if ev:
            E.count += 1
            evv = (E.key, E.count)
            E.last_ev_instr[E.count] = E.ninstr
            E.pending = False
        else:
            evv = (E.key, E.count + 1)
            E.pending = True
        E.ninstr += 1
        E.prog.append((waits, fn, E.key if ev else None, 1))
        for b in reads:
            if b.r.get(evv[0], 0) < evv[1]:
                b.r[evv[0]] = evv[1]
        for b in writes:
            b.w = evv
            b.r = {}
        return evv

    def dma(self, queue, semkey, kw, reads=(), writes=()):
        fn = ("dma_start", kw)
        E = self.engs[queue]
        deps = self._deps(reads, writes)
        prev = self.dma_val.get(semkey, 0)
        if prev:
            deps[semkey] = max(deps.get(semkey, 0), prev)
        waits = self._waits(E, deps)
        val = prev + 16
        self.dma_val[semkey] = val
        E.ninstr += 1
        E.prog.append((waits, fn, semkey, 16))
        evv = (semkey, val)
        for b in reads:
            if b.r.get(semkey, 0) < val:
                b.r[semkey] = val
        for b in writes:
            b.w = evv
            b.r = {}
        return evv

    def wait_all(self, eng, bufs):
        E = self.engs[eng]
        waits = self._waits(E, self._deps((), bufs))
        E.prog.append((waits, None, None, 0))

    def replay(self, eng, e, sems):
        for waits, fn, inckey, incv in self.engs[eng].prog:
            for k, v in waits:
                e.wait_ge(sems[k], v)
            if fn is not None:
                ins = getattr(e, fn[0])(**fn[1])
                if inckey is not None:
                    ins.then_inc(sems[inckey], incv)


def build_nc(nseq, S, nlayers, nr=NR):
    NJ = S // T
    NKB = S // 128
    nc = bass.Bass("TRN2", target_bir_lowering=False)
    xin = nc.dram_tensor("xin", [nseq, D, S], F32, kind="ExternalInput").ap()
    wpack = nc.dram_tensor("wpack", [nlayers, NSLOT, 128, SLOT], F32, kind="ExternalInput").ap()
    spk = nc.dram_tensor("spk", [nlayers, 128, NSP], F32, kind="ExternalInput").ap()
    cst = nc.dram_tensor("cst", [128, NCST + 128], F32, kind="ExternalInput").ap()
    out = nc.dram_tensor("out", [nseq, D, S], F32, kind="ExternalOutput").ap()
    wbf = [nc.dram_tensor("wbf%d" % l, [NSLOT, 128, SLOT], BF16, kind="Internal").ap() for l in range(nlayers)]
    xs = nc.dram_tensor("xs", [nseq, D, S], F32, kind="Internal").ap()

    P = Prog()
    for name in ("pe", "act", "dve", "pool"):
        P.add_engine(name, "e_" + name)
    P.add_engine("sp", None)
    semkeys = ["e_pe", "e_act", "e_dve", "e_pool"] + ["w%d" % i for i in range(nr)] + ["g%d" % i for i in range(8)] + ["c%d" % i for i in range(8)]

    with ExitStack() as es:
        def sb(name, shape, dt):
            return es.enter_context(nc.sbuf_tensor(name, shape, dt))

        sems = {k: es.enter_context(nc.semaphore(k)) for k in semkeys}

        xres = sb("xres", [128, DC, T], F32)
        hT = sb("hT", [128, DC, T], BF16)
        ym2 = sb("ym2", [128, 8, T], BF16)
        ybuf = sb("ybuf", [128, DC, T], F32)
        vreg = sb("vreg", [128, 16384], BF16)
        Kc = sb("Kc", [128, 4, S], BF16)
        Vc = sb("Vc", [128, NKB, 512], BF16)
        zconf = sb("zconf", [128, 4, 30 + T], BF16)
        yatt = sb("yatt", [128, 4, T], BF16)
        ident_bf = sb("ident_bf", [128, 128], BF16)
        dgr = sb("dgr", [128, 8, 128], BF16)
        cacc = sb("cacc", [128, 4, T], F32)
        ring = sb("ring", [128, nr, SLOT], BF16)
        rstd = sb("rstd", [128, T], F32)
        sqr = sb("sqr", [128, 2, T], BF16)
        spl = sb("spl", [128, NSP], F32)
        pw_bf = sb("pw_bf", [128, 512], BF16)
        wf_bf = sb("wf_bf", [128, 64], BF16)
        cstt = sb("cstt", [128, NCST], F32)
        ones_bf = sb("ones_bf", [128, 128], BF16)
        mask_bf = sb("mask_bf", [128, 128], BF16)
        Gcol = sb("Gcol", [128, NKB, 4], F32)
        Gst = sb("Gst", [128, NKB + 1, 4], F32)
        biasT = sb("biasT", [128, 4, NKB, 4], F32)
        zt = sb("zt", [128, 16], F32)
        nl = sb("nl", [128, 16], F32)
        phalo = sb("phalo", [128, 4, 16], F32)
        shalo = sb("shalo", [128, 4, 2], F32)
        epsc = sb("epsc", [128, 1], F32)
        onec = sb("onec", [128, 1], F32)

        psb = [es.enter_context(nc.psum_tensor("ps%d" % i, [128, T], F32)) for i in range(8)]

        off = [0]

        def carve(nbytes, dt, name):
            a = vreg[:, off[0] // 2:(off[0] + nbytes) // 2]
            if dt == F32:
                a = a.bitcast(F32)
            b = P.buf(name, "V", off[0], off[0] + nbytes)
            off[0] += nbytes
            return a, b

        qT_l = [carve(1024, BF16, "qT%d" % h) for h in range(4)]
        PT = [carve(1024, BF16, "PT%d" % i) for i in range(3)]
        gtmp = [carve(2048, F32, "gtmp%d" % i) for i in range(2)]
        ctmp = [carve(2048, F32, "ctmp%d" % i) for i in range(2)]
        zwork = [carve(2056, F32, "zwork%d" % i) for i in range(2)]
        cv = [carve(2048, F32, "cv%d" % i) for i in range(2)]
        pwork = carve(2112, F32, "pwork")
        st1 = carve(2112, F32, "st1")
        st2 = carve(2112, F32, "st2")
        rl = carve(2048, F32, "rl")
        assert off[0] <= 32768, off[0]
        uh = vreg[:, :].rearrange("p (f t) -> p f t", t=T)
        uh_b = [P.buf("uh%d" % f, "V", 1024 * f, 1024 * f + 1024) for f in range(32)]

        B_x = [P.buf("x%d" % i) for i in range(DC)]
        B_h = [P.buf("h%d" % i) for i in range(DC)]
        B_m2 = [P.buf("m2_%d" % i) for i in range(8)]
        B_y = [P.buf("y%d" % i) for i in range(DC)]
        B_K = [[P.buf("K%d_%d" % (h, j)) for j in range(NJ)] for h in range(4)]
        B_V = [P.buf("V%d" % j) for j in range(NJ)]
        B_zc = [P.buf("zc%d" % c) for c in range(4)]
        B_ca = [P.buf("ca%d" % c) for c in range(4)]
        B_ring = [P.buf("ring%d" % i) for i in range(nr)]
        B_rstd = P.buf("rstd")
        B_sq = [P.buf("sq0"), P.buf("sq1")]
        B_spl = P.buf("spl"); B_pw = P.buf("pw"); B_wf = P.buf("wf"); B_cst = P.buf("cst")
        B_G = P.buf("G"); B_bias = P.buf("bias"); B_zt = P.buf("zt"); B_nl = P.buf("nl")
        B_ph = P.buf("ph"); B_sh = P.buf("sh")
        B_ps = [P.buf("psum%d" % i) for i in range(8)]
        B_ya = [P.buf("ya%d" % h) for h in range(4)]
        B_dg = [P.buf("dg%d" % i) for i in range(8)]
        B_id = P.buf("ident")
        B_wbf = [[P.buf("wbf%d_%d" % (l, s)) for s in range(NSLOT)] for l in range(nlayers)]
        B_xs = [[[P.buf("xs%d_%d_%d" % (q, j, dc)) for dc in range(DC)] for j in range(NJ)] for q in range(nseq)]

        tri_f = cstt[:, O_TRI:O_TRI + 128]
        ones_f = cstt[:, O_ONE:O_ONE + 128]
        ymix = [yatt[:, i, :] for i in range(4)] + [hT[:, i, :] for i in range(4, 8)] + [ym2[:, i, :] for i in range(8)]
        B_ymix = B_ya + B_h[4:8] + B_m2

        def act(func, out, in_, reads, writes, **kw):
            return P.op("act", "activation", dict(out=out, in_=in_, func=func, **kw), reads, writes)

        def tt_(eng, out, in0, in1, op, reads, writes):
            return P.op(eng, "tensor_tensor", dict(out=out, in0=in0, in1=in1, op=op), reads, writes)

        def ts_(eng, out, in0, s1, op0, reads, writes, s2=None, op1=None):
            kw = dict(out=out, in0=in0, scalar1=s1, scalar2=s2, op0=op0)
            if op1 is not None:
                kw["op1"] = op1
            return P.op(eng, "tensor_scalar", kw, reads, writes)

        def stt(eng, out, in0, scalar, in1, op0, op1, reads, writes):
            assert eng == "dve"
            return P.op(eng, "scalar_tensor_tensor", dict(out=out, in0=in0, scalar=scalar, in1=in1, op0=op0, op1=op1), reads, writes)

        def mm(out, lhsT, rhs, start, stop, reads, writes, ev):
            return P.op("pe", "matmul", dict(out=out, lhsT=lhsT, rhs=rhs, start=start, stop=stop), reads, writes, ev=ev)

        cidx = [0]
        conv_next = {"l": 1, "s": 0}

        def emit_conversion(l, s):
            P.dma("pool", "c%d" % (cidx[0] % 8), dict(out=wbf[l][s], in_=wpack[l, s]), writes=[B_wbf[l][s]])
            cidx[0] += 1

        def convert_some(n):
            while n > 0 and conv_next["l"] < nlayers:
                emit_conversion(conv_next["l"], conv_next["s"])
                conv_next["s"] += 1
                if conv_next["s"] == NSLOT:
                    conv_next["s"] = 0
                    conv_next["l"] += 1
                n -= 1

        for s in range(NSLOT):
            emit_conversion(0, s)
        gi = [0]

        def gdma(out, in_, reads=(), writes=()):
            k = "g%d" % (gi[0] % 8)
            gi[0] += 1
            return P.dma("sp", k, dict(out=out, in_=in_), reads, writes)

        gdma(cstt[:], cst[:, 0:NCST], writes=[B_cst])
        P.dma("pool", "c7", dict(out=ident_bf[:], in_=cst[:, NCST:NCST + 128]), writes=[B_id])
        P.op("dve", "tensor_copy", dict(out=ones_bf[:], in_=ones_f), [B_cst], [B_cst])
        P.op("dve", "tensor_copy", dict(out=mask_bf[:], in_=tri_f), [B_cst], [B_cst])
        P.op("dve", "memset", dict(ap=epsc[:], constant=EPS), [], [B_cst])
        P.op("dve", "memset", dict(ap=onec[:], constant=1.0), [], [B_cst])

        plan = [(l, s) for l in range(nlayers) for q in range(nseq) for j in range(NJ) for s in range(NSLOT)]
        wq = {"n": 0, "issued": 0}

        def issue_upto(n):
            n = min(n, len(plan))
            while wq["issued"] < n:
                i = wq["issued"]
                l, s = plan[i]
                r = i % nr
                P.dma("sp", "w%d" % r, dict(out=ring[:, r, :], in_=wbf[l][s]), reads=[B_wbf[l][s]], writes=[B_ring[r]])
                wq["issued"] += 1

        def next_slot():
            i = wq["n"]
            wq["n"] += 1
            issue_upto(i + 1)
            r = i % nr
            return ring[:, r, :], B_ring[r], i

        def after_slot(i):
            issue_upto(i + nr + 1)

        issue_upto(nr)
        gen = [0]

        def gbank():
            b = gen[0] % 4
            gen[0] += 1
            return b

        sqi = [0]

        def nsq():
            k = sqi[0] % 2
            sqi[0] += 1
            return k

        def rms_stats(src_tile, src_bufs):
            for dc in range(DC):
                k = nsq()
                act(AF.Square, sqr[:, k, :], src_tile[:, dc, :], [src_bufs[dc]], [B_sq[k]])
                mm(psb[6][:], ones_bf[:], sqr[:, k, :], dc == 0, dc == DC - 1, [B_sq[k], B_cst], [B_ps[6]], True)
            act(AF.Ln, rstd[:], psb[6][:], [B_ps[6], B_cst], [B_rstd], bias=epsc[:], scale=1.0 / D)
            act(AF.Exp, rstd[:], rstd[:], [B_rstd], [B_rstd], scale=-0.5)

        def prenorm(goff):
            rms_stats(xres, B_x)
            for dc in range(DC):
                eng = "dve"
                stt(eng, hT[:, dc, :], xres[:, dc, :], spl[:, goff + dc:goff + dc + 1], rstd[:], ALU.mult, ALU.mult,
                    [B_x[dc], B_rstd, B_spl], [B_h[dc]])

        ystat_pend = []

        def ystat(dch, src_ap, src_bufs):
            ystat_flush()
            k = nsq()
            act(AF.Square, sqr[:, k, :], src_ap, src_bufs, [B_sq[k]])
            ystat_pend.append((dch, k))

        def ystat_flush():
            while ystat_pend:
                dch, k = ystat_pend.pop(0)
                mm(psb[6][:], ones_bf[:], sqr[:, k, :], dch == 0, dch == DC - 1, [B_sq[k], B_cst], [B_ps[6]], True)

        def rstd_from_stats():
            ystat_flush()
            act(AF.Ln, rstd[:], psb[6][:], [B_ps[6], B_cst], [B_rstd], bias=epsc[:], scale=1.0 / D)
            act(AF.Exp, rstd[:], rstd[:], [B_rstd], [B_rstd], scale=-0.5)

        def postnorm_residual(goff, dst_tile, dst_bufs):
            rstd_from_stats()
            for dc in range(DC):
                tt_("pool" if dc % 3 == 2 else "dve", ybuf[:, dc, :], ybuf[:, dc, :], rstd[:], ALU.mult, [B_y[dc], B_rstd], [B_y[dc]])
            for dc in range(DC):
                eng = "dve"
                stt(eng, dst_tile[:, dc, :], ybuf[:, dc, :], spl[:, goff + dc:goff + dc + 1], xres[:, dc, :], ALU.mult, ALU.add,
                    [B_y[dc], B_x[dc], B_spl], [dst_bufs[dc]])

        def wgroup_mm(slot_ap, slot_buf, i2, rhs_list, rhs_bufs, bank, order=None):
            wv = slot_ap.rearrange("p (k c) -> p k c", c=256)
            n = len(rhs_list)
            order = list(range(n)) if order is None else order
            for i, kc in enumerate(order):
                mm(psb[bank][:], wv[:, kc, i2 * 128:(i2 + 1) * 128], rhs_list[kc], i == 0, i == n - 1,
                   [slot_buf, rhs_bufs[kc]], [B_ps[bank]], i == n - 1)

        hlist = [hT[:, dc, :] for dc in range(DC)]

        for l in range(nlayers):
            gdma(spl[:], spk[l], writes=[B_spl])
            P.op("dve", "tensor_copy", dict(out=pw_bf[:], in_=spl[:, O_PW:O_PW + 512]), [B_spl], [B_pw])
            P.op("dve", "tensor_copy", dict(out=wf_bf[:], in_=spl[:, O_WF:O_WF + 64]), [B_spl], [B_wf])
            src = xin if l == 0 else xs
            dst = out if l == nlayers - 1 else xs
            for q in range(nseq):
                P.op("pool", "memset", dict(ap=zconf[:, :, 0:30], constant=0.0), [], B_zc)
                P.op("pool", "memset", dict(ap=phalo[:], constant=0.0), [], [B_ph])
                P.op("pool", "memset", dict(ap=shalo[:], constant=0.0), [], [B_sh])
                P.op("pool", "memset", dict(ap=Gst[:, 0, :], constant=0.0), [], [B_G])
                for j in range(NJ):
                    t0 = j * T
                    sv = src[q].rearrange("(dc p) t -> p dc t", p=128)
                    for dc in range(DC):
                        rd = [] if l == 0 else [B_xs[q][j][dc]]
                        gdma(xres[:, dc, :], sv[:, dc, t0:t0 + T], reads=rd, writes=[B_x[dc]])
                    if conv_next["l"] <= l + 1:
                        convert_some(-(-NSLOT // (nseq * NJ)))
                    prenorm(O_G1)
                    for tt in range(4):
                        for dc in range(DC):
                            mm(psb[7][:, tt * 4:tt * 4 + 4], hT[:, dc, tt * 128:(tt + 1) * 128], wf_bf[:, dc * 4:dc * 4 + 4],
                               dc == 0, dc == DC - 1, [B_h[dc], B_wf], [B_ps[7]], dc == DC - 1)
                    tt_("dve", zt[:], psb[7][:, 0:16], spl[:, O_BF:O_BF + 16], ALU.add, [B_ps[7], B_spl], [B_zt])
                    act(AF.Exp, nl[:], zt[:], [B_zt], [B_nl], scale=-1.0)
                    act(AF.Ln, nl[:], nl[:], [B_nl, B_cst], [B_nl], bias=onec[:], scale=1.0)
                    for tt in range(4):
                        mm(psb[7][:, 32 + tt * 4:36 + tt * 4], tri_f, nl[:, tt * 4:tt * 4 + 4], True, True, [B_nl, B_cst], [B_ps[7]], False)
                        mm(psb[7][:, 64 + tt * 4:68 + tt * 4], ones_f, nl[:, tt * 4:tt * 4 + 4], True, True, [B_nl, B_cst], [B_ps[7]], tt == 3)
                    for tt in range(4):
                        qb = 4 * j + tt
                        tt_("dve", Gcol[:, qb, :], psb[7][:, 32 + tt * 4:36 + tt * 4], Gst[:, qb, :], ALU.add, [B_ps[7], B_G], [B_G])
                        tt_("dve", Gst[:, qb + 1, :], psb[7][:, 64 + tt * 4:68 + tt * 4], Gst[:, qb, :], ALU.add, [B_ps[7], B_G], [B_G])
                    for tt in range(4):
                        qb = 4 * j + tt
                        tt_("dve", biasT[:, tt, 0:qb + 1, :], Gcol[:, 0:qb + 1, :], Gst[:, qb:qb + 1, :].broadcast_to([128, qb + 1, 4]),
                            ALU.subtract, [B_G], [B_bias])

                    nkb = 4 * j + 4
                    vslots = []
                    attq = []
                    conv_pending = []
                    st = {"conv_done": 0, "ln": False, "p_done": False, "pti": 0, "dgi": 0}
                    deferred_final = []

                    def emit_conv_pe(c):
                        bank = gbank()
                        o = O_CDW + 31 * c
                        for k in range(31):
                            i = st["dgi"] % 8
                            st["dgi"] += 1
                            ts_("dve", dgr[:, i, :], ident_bf[:], spl[:, o + k:o + k + 1], ALU.mult, [B_id, B_spl], [B_dg[i]])
                            mm(psb[bank][:], dgr[:, i, :], zconf[:, c, k:k + T], k == 0, k == 30, [B_dg[i], B_zc[c]], [B_ps[bank]], True)
                        act(AF.Copy, cacc[:, c, :], psb[bank][:], [B_ps[bank]], [B_ca[c]])
                        P.op("pool", "tensor_copy", dict(out=zconf[:, c, 0:30], in_=zconf[:, c, T:T + 30]), [B_zc[c]], [B_zc[c]])
                        st["conv_done"] += 1

                    ln_stages = []

                    def emit_ln():
                        mean, b_mean = st1[0][:, 0:T], st1[1]
                        var, b_var = st2[0][:, 0:T], st2[1]
                        tmpl, b_tmpl = pwork[0][:, 0:T], pwork[1]

                        def s0():
                            for c in range(4):
                                k = nsq()
                                act(AF.Copy, sqr[:, k, :], cacc[:, c, :], [B_ca[c]], [B_sq[k]])
                                mm(psb[6][:], ones_bf[:], sqr[:, k, :], c == 0, c == 3, [B_sq[k], B_cst], [B_ps[6]], True)
                            for c in range(4):
                                k = nsq()
                                act(AF.Square, sqr[:, k, :], cacc[:, c, :], [B_ca[c]], [B_sq[k]])
                                mm(psb[7][:], ones_bf[:], sqr[:, k, :], c == 0, c == 3, [B_sq[k], B_cst], [B_ps[7]], True)

                        def s1():
                            act(AF.Copy, mean, psb[6][:], [B_ps[6]], [b_mean], scale=1.0 / 512)
                            act(AF.Square, tmpl, psb[6][:], [B_ps[6]], [b_tmpl], scale=1.0 / 512)

                        def s2():
                            stt("dve", var, psb[7][:], 1.0 / 512, tmpl, ALU.mult, ALU.subtract, [B_ps[7], b_tmpl], [b_var])
                            act(AF.Ln, var, var, [b_var, B_cst], [b_var], bias=epsc[:], scale=1.0)
                            act(AF.Exp, var, var, [b_var], [b_var], scale=-0.5)

                        def s3():
                            for c in range(4):
                                tt_("dve", cacc[:, c, :], cacc[:, c, :], mean, ALU.subtract, [B_ca[c], b_mean], [B_ca[c]])
                            for c in range(4):
                                tt_("dve", cacc[:, c, :], cacc[:, c, :], var, ALU.mult, [B_ca[c], b_var], [B_ca[c]])
                            for c in range(4):
                                ts_("dve", cacc[:, c, :], cacc[:, c, :], spl[:, O_LNG + c:O_LNG + c + 1], ALU.mult, [B_ca[c], B_spl], [B_ca[c]],
                                    s2=spl[:, O_LNB + c:O_LNB + c + 1], op1=ALU.add)

                        def sig(cs):
                            def f():
                                for c in cs:
                                    ga, gb_ = gtmp[c % 2]
                                    act(AF.Exp, ga, cacc[:, c, :], [B_ca[c]], [gb_], scale=-1.0)
                                for c in cs:
                                    ga, gb_ = gtmp[c % 2]
                                    act(AF.Ln, ga, ga, [gb_, B_cst], [gb_], bias=onec[:], scale=1.0)
                                for c in cs:
                                    ga, gb_ = gtmp[c % 2]
                                    act(AF.Exp, ga, ga, [gb_], [gb_], scale=-1.0)
                            return f

                        def fin(cs):
                            def f():
                                for c in cs:
                                    ga, gb_ = gtmp[c % 2]
                                    tt_("dve", cacc[:, c, :], cacc[:, c, :], ga, ALU.mult, [B_ca[c], gb_], [B_ca[c]])
                                    deferred_final.append(c)
                            return f

                        def s5():
                            fin((0, 1))()
                            sig((2, 3))()

                        ln_stages.extend([lambda: None, s0, s1, s2, s3, sig((0, 1)), s5, fin((2, 3))])
                        st["ln"] = True

                    def att_A(h, kb):
                        qa, qbuf = qT_l[h]
                        r = kb - 4 * j
                        c0 = max(r, 0) * 128
                        bank = gbank()
                        mm(psb[bank][:, c0:T], Kc[:, h, kb * 128:(kb + 1) * 128], qa[:, c0:T], True, True,
                           [B_K[h][kb // 4], qbuf], [B_ps[bank]], True)
                        pa, pbf = PT[st["pti"] % 3]
                        st["pti"] += 1
                        for tt in range(max(r, 0), 4):
                            act(AF.Exp, pa[:, tt * 128:(tt + 1) * 128], psb[bank][:, tt * 128:(tt + 1) * 128], [B_ps[bank], B_bias], [pbf],
                                bias=biasT[:, tt, kb, h:h + 1], scale=1.0)
                        if r >= 0:
                            tt_("dve", pa[:, r * 128:(r + 1) * 128], pa[:, r * 128:(r + 1) * 128], mask_bf[:], ALU.mult, [pbf, B_cst], [pbf])
                        return (h, kb, c0, pa, pbf)

                    def att_B(h, kb, c0, pa, pbf):
                        mm(psb[4][:, c0:T], Vc[:, kb, h * 128:(h + 1) * 128], pa[:, c0:T], kb == 0, kb == nkb - 1,
                           [B_V[kb // 4], pbf], [B_ps[4]], False)
                        mm(psb[5][:, c0:T], ones_bf[:], pa[:, c0:T], kb == 0, kb == nkb - 1, [pbf, B_cst], [B_ps[5]], True)
                        if kb == nkb - 1:
                            ra, rb = rl
                            act(AF.Ln, ra, psb[5][:], [B_ps[5]], [rb])
                            act(AF.Exp, ra, ra, [rb], [rb], scale=-1.0)
                            tt_("dve", yatt[:, h, :], psb[4][:], ra, ALU.mult, [B_ps[4], B_ps[5], rb], [B_ya[h]])

                    pendB = []

                    def drain_att(n, flush=False):
                        while n > 0 and attq:
                            h, kb = attq.pop(0)
                            a = att_A(h, kb)
                            if pendB:
                                att_B(*pendB.pop(0))
                            pendB.append(a)
                            n -= 1
                        if flush:
                            while pendB:
                                att_B(*pendB.pop(0))

                    nslots = len(WIN_SLOTS)
                    for sidx, (kind, it) in enumerate(WIN_SLOTS):
                        sl, slb, si = next_slot()
                        if kind == "V":
                            vslots.append((sl, slb, si))
                            if len(vslots) == 2:
                                for tt in range(4):
                                    bank = gbank()
                                    for dc in range(DC):
                                        sl2, slb2, _ = vslots[dc // 8]
                                        wv = sl2.rearrange("p (k c) -> p k c", c=512)
                                        mm(psb[bank][:], hT[:, dc, tt * 128:(tt + 1) * 128], wv[:, dc % 8, :], dc == 0, dc == DC - 1,
                                           [B_h[dc], slb2], [B_ps[bank]], dc == DC - 1)
                                    act(AF.Copy, Vc[:, 4 * j + tt, :], psb[bank][:], [B_ps[bank]], [B_V[j]])
                                after_slot(vslots[0][2])
                                after_slot(vslots[1][2])
                            continue
                        for i2, (kd, ci) in enumerate(it):
                            bank = gbank()
                            wgroup_mm(sl, slb, i2, hlist, B_h, bank)
                            pb = psb[bank][:]
                            Bp = B_ps[bank]
                            if kd == "k":
                                act(AF.Copy, Kc[:, ci, t0:t0 + T], pb, [Bp], [B_K[ci][j]])
                            elif kd == "q":
                                act(AF.Copy, qT_l[ci][0], pb, [Bp], [qT_l[ci][1]], scale=float(SCALE))
                                if ci == 3:
                                    attq.extend((h, kb) for h in range(4) for kb in range(nkb))
                            elif kd == "g":
                                ga, gb_ = gtmp[ci % 2]
                                act(AF.Exp, ga, pb, [Bp], [gb_], scale=-1.0)
                                act(AF.Ln, ga, ga, [gb_, B_cst], [gb_], bias=onec[:], scale=1.0)
                                act(AF.Exp, ga, ga, [gb_], [gb_], scale=-1.0)
                            elif kd == "a":
                                ga, gb_ = gtmp[ci % 2]
                                tt_("dve", zconf[:, ci, 30:30 + T], pb, ga, ALU.mult, [Bp, gb_], [B_zc[ci]])
                                conv_pending.append((ci, sidx))
                            elif kd == "p":
                                pa, pbuf = pwork
                                w = (2, 4, 8, 16)[ci]
                                P.op("pool", "tensor_copy", dict(out=pa[:, 0:16], in_=phalo[:, ci, :]), [B_ph], [pbuf])
                                act(AF.Copy, pa[:, 16:16 + T], pb, [Bp], [pbuf])
                                P.op("pool", "tensor_copy", dict(out=phalo[:, ci, :], in_=pa[:, T:T + 16]), [pbuf], [B_ph])
                                cur, curb = pa, pbuf
                                tmps = [st1, st2]
                                sh = 1
                                ti = 0
                                while sh < w:
                                    na, nb = tmps[ti % 2]
                                    ti += 1
                                    tt_("dve", na[:, sh:16 + T], cur[:, sh:16 + T], cur[:, 0:16 + T - sh], ALU.add, [curb], [nb])
                                    cur, curb = na, nb
                                    sh *= 2
                                oa, ob = tmps[ti % 2]
                                pooled = oa.bitcast(BF16)[:, 0:T]
                                stt("dve", pooled, cur[:, 16:16 + T], 1.0 / w, pa[:, 16:16 + T], ALU.mult, ALU.subtract, [curb, pbuf], [ob])
                                if j == 0:
                                    tt_("dve", cur[:, 16:32], cur[:, 16:32], cstt[:, O_INV + 16 * ci:O_INV + 16 * ci + 16], ALU.mult, [curb, B_cst], [curb])
                                    tt_("dve", pooled[:, 0:16], cur[:, 16:32], pa[:, 16:32], ALU.subtract, [curb, pbuf], [ob])
                                mm(psb[7][:], pw_bf[:, ci * 128:(ci + 1) * 128], pooled, True, True, [ob, B_pw], [B_ps[7]], True)
                                act(AF.Copy, ym2[:, ci, :], psb[7][:], [B_ps[7], B_spl], [B_m2[ci]], scale=spl[:, O_PSC + ci:O_PSC + ci + 1])
                                if ci == 3:
                                    st["p_done"] = True
                            elif kd == "C":
                                ca, cb = ctmp[ci % 2]
                                act(AF.Copy, ca, pb, [Bp], [cb])
                            elif kd == "H":
                                ca, cb = ctmp[ci % 2]
                                za, zb = zwork[ci % 2]
                                P.op("pool", "tensor_copy", dict(out=za[:, 0:2], in_=shalo[:, ci, :]), [B_sh], [zb])
                                tt_("dve", za[:, 2:2 + T], pb, ca, ALU.mult, [Bp, cb], [zb])
                                P.op("pool", "tensor_copy", dict(out=shalo[:, ci, :], in_=za[:, T:T + 2]), [zb], [B_sh])
                                va, vb = cv[ci % 2]
                                o = O_SDW + 3 * ci
                                ts_("dve", va, za[:, 0:T], spl[:, o:o + 1], ALU.mult, [zb, B_spl], [vb])
                                for k in (1, 2):
                                    stt("dve", va, za[:, k:k + T], spl[:, o + k:o + k + 1], va, ALU.mult, ALU.add, [zb, vb, B_spl], [vb])
                            elif kd == "B":
                                va, vb = cv[ci % 2]
                                tt_("dve", ym2[:, 4 + ci, :], pb, va, ALU.mult, [Bp, vb], [B_m2[4 + ci]])
                        after_slot(si)
                        while conv_pending and conv_pending[0][1] < sidx:
                            emit_conv_pe(conv_pending.pop(0)[0])
                        if st["conv_done"] == 4 and st["p_done"] and not st["ln"]:
                            emit_ln()
                        if ln_stages:
                            ln_stages.pop(0)()
                        left = nslots - 1 - sidx
                        if attq:
                            drain_att(-(-len(attq) // max(left, 1)) if left > 0 else len(attq))
                    while conv_pending:
                        emit_conv_pe(conv_pending.pop(0)[0])
                    if not st["ln"]:
                        emit_ln()
                    drain_att(10 ** 6, flush=True)
                    while ln_stages:
                        ln_stages.pop(0)()
                    for c in deferred_final:
                        P.op("act", "activation", dict(out=hT[:, 4 + c, :], in_=cacc[:, c, :], func=AF.Copy), [B_ca[c]], [B_h[4 + c]])

                    for g in range(8):
                        sl, slb, si = next_slot()
                        for i2 in range(2):
                            bank = gbank()
                            dch = 2 * g + i2
                            wgroup_mm(sl, slb, i2, ymix, B_ymix, bank, order=list(range(8, 16)) + list(range(4, 8)) + list(range(0, 4)))
                            act(AF.Copy, ybuf[:, dch, :], psb[bank][:], [B_ps[bank]], [B_y[dch]])
                            ystat(dch, psb[bank][:], [B_ps[bank]])
                        after_slot(si)
                    postnorm_residual(O_G2, xres, B_x)
                    prenorm(O_G3)
                    for hf in range(2):
                        for g in range(16):
                            sl, slb, si = next_slot()
                            for i2 in range(2):
                                bank = gbank()
                                fl = 2 * g + i2
                                wgroup_mm(sl, slb, i2, hlist, B_h, bank)
                                act(AF.Relu, uh[:, fl, :], psb[bank][:], [B_ps[bank]], [uh_b[fl]])
                                eng = "dve" if fl % 2 == 0 else "pool"
                                tt_(eng, uh[:, fl, :], uh[:, fl, :], uh[:, fl, :], ALU.mult, [uh_b[fl]], [uh_b[fl]])
                            after_slot(si)
                        for dch in range(16):
                            sl, slb, si = next_slot()
                            bank = gbank()
                            wv = sl.rearrange("p (k c) -> p k c", c=128)
                            for fc in range(32):
                                mm(psb[bank][:], wv[:, fc, :], uh[:, fc, :], fc == 0, fc == 31, [slb, uh_b[fc]], [B_ps[bank]], fc == 31)
                            if hf == 0:
                                act(AF.Copy, ybuf[:, dch, :], psb[bank][:], [B_ps[bank]], [B_y[dch]])
                            else:
                                tt_("dve", ybuf[:, dch, :], psb[bank][:], ybuf[:, dch, :], ALU.add, [B_ps[bank], B_y[dch]], [B_y[dch]])
                                ystat(dch, ybuf[:, dch, :], [B_y[dch]])
                            after_slot(si)
                    postnorm_residual(O_G4, ybuf, B_y)
                    dv = dst[q].rearrange("(dc p) t -> p dc t", p=128)
                    for dc in range(DC):
                        wr = [] if l == nlayers - 1 else [B_xs[q][j][dc]]
                        gdma(dv[:, dc, t0:t0 + T], ybuf[:, dc, :], reads=[B_y[dc]], writes=wr)
        P.wait_all("sp", B_y)
        assert not P.engs["pe"].pending
        assert wq["n"] == len(plan) and wq["issued"] == len(plan), (wq, len(plan))

        with nc.Block() as block:
            @block.sync
            def _(e):
                P.replay("sp", e, sems)

            @block.tensor
            def _(e):
                P.replay("pe", e, sems)

            @block.scalar
            def _(e):
                P.replay("act", e, sems)

            @block.vector
            def _(e):
                P.replay("dve", e, sems)

            @block.gpsimd
            def _(e):
                P.replay("pool", e, sems)
    stats = {k: (E.ninstr, E.count) for k, E in P.engs.items()}
    return nc, stats


N_CORES = 8
LAYERS_PER_LAUNCH = 4

_cache = {}


def _get_nc(nseq, S, nl):
    key = (nseq, S, nl)
    if key not in _cache:
        _cache[key] = build_nc(nseq, S, nl)[0]
    return _cache[key]


def run_layers(xT_shards, params, layers, S):
    nl = len(layers)
    nseq = xT_shards[0].shape[0]
    nc = _get_nc(nseq, S, nl)
    wp = np.stack([pack_weights_layer(params["w_in"][l], params["w_out"][l], params["w_mlp1"][l], params["w_mlp2"][l]) for l in layers])
    sp = np.stack([pack_small_layer(l, params) for l in layers])
    cs = make_consts()
    in_maps = [{"xin": np.ascontiguousarray(xs), "wpack": wp, "spk": sp, "cst": cs} for xs in xT_shards]
    res = run_bass_kernel_spmd(nc, in_maps, core_ids=list(range(len(xT_shards))))
    return [np.asarray(r["out"]) for r in res.results]


def kernel(**inputs):
    p = {k: np.asarray(v) for k, v in inputs.items()}
    x = p["x"]
    B, S, _ = x.shape
    depth = p["w_in"].shape[0]
    xT = np.ascontiguousarray(x.transpose(0, 2, 1))
    per = B // N_CORES
    shards = [xT[c * per:(c + 1) * per] for c in range(N_CORES)]
    for l0 in range(0, depth, LAYERS_PER_LAUNCH):
        shards = run_layers(shards, p, list(range(l0, min(depth, l0 + LAYERS_PER_LAUNCH))), S)
    outT = np.concatenate(shards, axis=0)
    return np.ascontiguousarray(outT.transpose(0, 2, 1)).astype(np.float32)
```

```python
import numpy as np
from contextlib import ExitStack
import concourse.bass as bass
import concourse.mybir as mybir
from concourse.bass_utils import run_bass_kernel_spmd

F32 = mybir.dt.float32
BF16 = mybir.dt.bfloat16
AF = mybir.ActivationFunctionType
ALU = mybir.AluOpType

D = 2048
DC = 16
T = 512
DFF = 8192
EPS = 1e-6
NSLOT = 90
SLOT = 4096
NR = 3
SCALE = 1.0 / np.sqrt(128.0)

O_G1, O_G2, O_G3, O_G4 = 0, 16, 32, 48
O_CDW = 64
O_LNG = O_CDW + 124
O_LNB = O_LNG + 4
O_PSC = O_LNB + 4
O_SDW = O_PSC + 4
O_BF = O_SDW + 12
O_WF = O_BF + 16
O_PW = O_WF + 64
NSP = O_PW + 512
O_TRI, O_ONE, O_INV = 0, 128, 256
NCST = 320

def _col(kind, i):
    base = {"q": 0, "k": 512, "a": 1540, "g": 2052, "p": 2564, "B": 3076, "C": 3588, "H": 4100}[kind]
    return base + 128 * i

WIN_ORDER = ([("k", i) for i in range(4)] + ["V"] + [("q", i) for i in range(4)]
             + [x for c in range(4) for x in (("g", c), ("a", c))]
             + [("p", i) for i in range(4)]
             + [x for c in range(4) for x in (("C", c), ("H", c), ("B", c))])


def _win_slots():
    slots = []
    pend = []
    for it in WIN_ORDER:
        if it == "V":
            assert not pend
            slots.append(("V", 0))
            slots.append(("V", 1))
        else:
            pend.append(it)
            if len(pend) == 2:
                slots.append(("W", pend))
                pend = []
    assert not pend and len(slots) == 18
    return slots


WIN_SLOTS = _win_slots()


def pack_weights_layer(w_in, w_out, w1, w2):
    out = np.empty((NSLOT, 128, SLOT), np.float32)
    s = 0
    wi = w_in.reshape(DC, 128, -1)
    for kind, it in WIN_SLOTS:
        if kind == "V":
            blk = wi[it * 8:(it + 1) * 8, :, 1024:1536]
            out[s] = blk.transpose(1, 0, 2).reshape(128, SLOT)
        else:
            cols = np.concatenate([np.arange(_col(k, i), _col(k, i) + 128) for k, i in it])
            out[s] = wi[:, :, cols].transpose(1, 0, 2).reshape(128, SLOT)
        s += 1
    wo = w_out.reshape(DC, 128, D)
    for g in range(8):
        out[s] = wo[:, :, 256 * g:256 * g + 256].transpose(1, 0, 2).reshape(128, SLOT)
        s += 1
    w1r = w1.reshape(DC, 128, DFF)
    w2r = w2.reshape(64, 128, D)
    for hf in range(2):
        for g in range(16):
            gg = hf * 16 + g
            out[s] = w1r[:, :, 256 * gg:256 * gg + 256].transpose(1, 0, 2).reshape(128, SLOT)
            s += 1
        for dch in range(16):
            out[s] = w2r[hf * 32:(hf + 1) * 32, :, dch * 128:(dch + 1) * 128].transpose(1, 0, 2).reshape(128, SLOT)
            s += 1
    assert s == NSLOT
    return out


def pack_small_layer(l, p):
    sp = np.zeros((128, NSP), np.float32)
    for o, name in ((O_G1, "mix_norm_pre"), (O_G2, "mix_norm_post"), (O_G3, "mlp_norm_pre"), (O_G4, "mlp_norm_post")):
        sp[:, o:o + 16] = p[name][l].reshape(16, 128).T
    sp[:, O_CDW:O_CDW + 124] = p["conf_dw"][l].reshape(31, 4, 128).transpose(2, 1, 0).reshape(128, 124)
    sp[:, O_LNG:O_LNG + 4] = p["conf_ln_g"][l].reshape(4, 128).T
    sp[:, O_LNB:O_LNB + 4] = p["conf_ln_b"][l].reshape(4, 128).T
    sp[:, O_PSC:O_PSC + 4] = p["pool_scale"][l].reshape(4, 128).T
    sp[:, O_SDW:O_SDW + 12] = p["sc_dw"][l].reshape(3, 4, 128).transpose(2, 1, 0).reshape(128, 12)
    sp[:, O_BF:O_BF + 16] = np.tile(p["b_forget"][l][None, :], (128, 4))
    sp[:, O_WF:O_WF + 64] = p["w_in"][l][:, 1536:1540].reshape(16, 128, 4).transpose(1, 0, 2).reshape(128, 64)
    sp[:, O_PW:O_PW + 512] = p["pool_w"][l].transpose(1, 0, 2).reshape(128, 512)
    return sp


def make_consts():
    c = np.zeros((128, NCST + 128), np.float32)
    c[:, NCST:NCST + 128] = np.eye(128, dtype=np.float32)
    s = np.arange(128)
    c[:, O_TRI:O_TRI + 128] = (s[:, None] <= s[None, :]).astype(np.float32)
    c[:, O_ONE:O_ONE + 128] = 1.0
    for g, w in enumerate((2, 4, 8, 16)):
        c[:, O_INV + 16 * g:O_INV + 16 * g + 16] = 1.0 / np.minimum(np.arange(16) + 1, w)
    return c


class Buf:
    __slots__ = ("name", "w", "r", "region", "lo", "hi", "ov")

    def __init__(self, name, region=None, lo=0, hi=0):
        self.name = name
        self.w = None
        self.r = {}
        self.region = region
        self.lo, self.hi = lo, hi
        self.ov = [self]


class Eng:
    def __init__(self, name, key):
        self.name = name
        self.key = key
        self.count = 0
        self.ninstr = 0
        self.last_ev_instr = {}
        self.seen = {}
        self.prog = []
        self.pending = False


class Prog:
    def __init__(self):
        self.engs = {}
        self.regions = {}
        self.dma_val = {}

    def add_engine(self, name, key):
        self.engs[name] = Eng(name, key)

    def buf(self, name, region=None, lo=0, hi=0):
        b = Buf(name, region, lo, hi)
        if region is not None:
            lst = self.regions.setdefault(region, [])
            for o in lst:
                if o.lo < hi and lo < o.hi:
                    o.ov.append(b)
                    b.ov.append(o)
            lst.append(b)
        return b

    def _deps(self, reads, writes):
        deps = {}

        def add(ev):
            if ev is not None:
                k, v = ev
                if deps.get(k, 0) < v:
                    deps[k] = v
        for b0 in reads:
            for b in b0.ov:
                add(b.w)
        for b0 in writes:
            for b in b0.ov:
                add(b.w)
                for k, v in b.r.items():
                    add((k, v))
        return deps

    def _waits(self, E, deps):
        waits = []
        for k, v in deps.items():
            if k == E.key:
                if E.name == "pe":
                    continue
                if E.ninstr - E.last_ev_instr.get(v, -10) > 2:
                    continue
            if E.seen.get(k, 0) >= v:
                continue
            if k.startswith("e_"):
                assert v <= self.engs[k[2:]].count, ("wait on future event", E.name, k, v)
            E.seen[k] = v
            waits.append((k, v))
        return waits

    def op(self, eng, meth, kw, reads=(), writes=(), ev=True):
        fn = (meth, kw)
        E = self.engs[eng]
        waits = self._waits(E, self._deps(reads, writes))
        if ev:
            E.count += 1
            evv = (E.key, E.count)
            E.last_ev_instr[E.count] = E.ninstr
            E.pending = False
        else:
            evv = (E.key, E.count + 1)
            E.pending = True
        E.ninstr += 1
        E.prog.append((waits, fn, E.key if ev else None, 1))
        for b in reads:
            if b.r.get(evv[0], 0) < evv[1]:
                b.r[evv[0]] = evv[1]
        for b in writes:
            b.w = evv
            b.r = {}
        return evv

    def dma(self, queue, semkey, kw, reads=(), writes=()):
        fn = ("dma_start", kw)
        E = self.engs[queue]
        deps = self._deps(reads, writes)
        prev = self.dma_val.get(semkey, 0)
        if prev:
            deps[semkey] = max(deps.get(semkey, 0), prev)
        waits = self._waits(E, deps)
        val = prev + 16
        self.dma_val[semkey] = val
        E.ninstr += 1
        E.prog.append((waits, fn, semkey, 16))
        evv = (semkey, val)
        for b in reads:
            if b.r.get(semkey, 0) < val:
                b.r[semkey] = val
        for b in writes:
            b.w = evv
            b.r = {}
        return evv

    def wait_all(self, eng, bufs):
        E = self.engs[eng]
        waits = self._waits(E, self._deps((), bufs))
        E.prog.append((waits, None, None, 0))

    def replay(self, eng, e, sems):
        for waits, fn, inckey, incv in self.engs[eng].prog:
            for k, v in waits:
                e.wait_ge(sems[k], v)
            if fn is not None:
                ins = getattr(e, fn[0])(**fn[1])
                if inckey is not None:
                    ins.then_inc(sems[inckey], incv)


def build_nc(nseq, S, nlayers, nr=NR):
    NJ = S // T
    NKB = S // 128
    nc = bass.Bass("TRN2", target_bir_lowering=False)
    xin = nc.dram_tensor("xin", [nseq, D, S], F32, kind="ExternalInput").ap()
    wpack = nc.dram_tensor("wpack", [nlayers, NSLOT, 128, SLOT], F32, kind="ExternalInput").ap()
    spk = nc.dram_tensor("spk", [nlayers, 128, NSP], F32, kind="ExternalInput").ap()
    cst = nc.dram_tensor("cst", [128, NCST + 128], F32, kind="ExternalInput").ap()
    out = nc.dram_tensor("out", [nseq, D, S], F32, kind="ExternalOutput").ap()
    wbf = [nc.dram_tensor("wbf%d" % l, [NSLOT, 128, SLOT], BF16, kind="Internal").ap() for l in range(nlayers)]
    xs = nc.dram_tensor("xs", [nseq, D, S], F32, kind="Internal").ap()

    P = Prog()
    for name in ("pe", "act", "dve", "pool"):
        P.add_engine(name, "e_" + name)
    P.add_engine("sp", None)
    semkeys = ["e_pe", "e_act", "e_dve", "e_pool"] + ["w%d" % i for i in range(nr)] + ["g%d" % i for i in range(8)] + ["c%d" % i for i in range(8)]

    with ExitStack() as es:
        def sb(name, shape, dt):
            return es.enter_context(nc.sbuf_tensor(name, shape, dt))

        sems = {k: es.enter_context(nc.semaphore(k)) for k in semkeys}

        xres = sb("xres", [128, DC, T], F32)
        hT = sb("hT", [128, DC, T], BF16)
        ym2 = sb("ym2", [128, 8, T], BF16)
        ybuf = sb("ybuf", [128, DC, T], F32)
        vreg = sb("vreg", [128, 16384], BF16)
        Kc = sb("Kc", [128, 4, S], BF16)
        Vc = sb("Vc", [128, NKB, 512], BF16)
        zconf = sb("zconf", [128, 4, 30 + T], BF16)
        yatt = sb("yatt", [128, 4, T], BF16)
        ident_bf = sb("ident_bf", [128, 128], BF16)
        dgr = sb("dgr", [128, 4, 128], BF16)
        cacc = sb("cacc", [128, 4, T], F32)
        ring = sb("ring", [128, nr, SLOT], BF16)
        rstd = sb("rstd", [128, T], F32)
        rstd2 = sb("rstd2", [128, T], F32)
        sqr = sb("sqr", [128, 2, T], BF16)
        spl = sb("spl", [128, NSP], F32)
        pw_bf = sb("pw_bf", [128, 512], BF16)
        wf_bf = sb("wf_bf", [128, 64], BF16)
        cstt = sb("cstt", [128, NCST], F32)
        ones_bf = sb("ones_bf", [128, 128], BF16)
        mask_bf = sb("mask_bf", [128, 128], BF16)
        Gcol = sb("Gcol", [128, NKB, 4], F32)
        Gst = sb("Gst", [128, NKB + 1, 4], F32)
        biasT = sb("biasT", [128, 4, NKB, 4], F32)
        zt = sb("zt", [128, 16], F32)
        nl = sb("nl", [128, 16], F32)
        phalo = sb("phalo", [128, 4, 16], F32)
        shalo = sb("shalo", [128, 4, 2], F32)
        epsc = sb("epsc", [128, 1], F32)
        onec = sb("onec", [128, 1], F32)

        psb = [es.enter_context(nc.psum_tensor("ps%d" % i, [128, T], F32)) for i in range(8)]

        off = [0]

        def carve(nbytes, dt, name):
            a = vreg[:, off[0] // 2:(off[0] + nbytes) // 2]
            if dt == F32:
                a = a.bitcast(F32)
            b = P.buf(name, "V", off[0], off[0] + nbytes)
            off[0] += nbytes
            return a, b

        qT_l = [carve(1024, BF16, "qT%d" % h) for h in range(4)]
        PT = [carve(1024, BF16, "PT%d" % i) for i in range(3)]
        gtmp = [carve(2048, F32, "gtmp%d" % i) for i in range(2)]
        ctmp = [carve(2048, F32, "ctmp%d" % i) for i in range(2)]
        zwork = [carve(2056, F32, "zwork%d" % i) for i in range(2)]
        cv = [carve(2048, F32, "cv%d" % i) for i in range(2)]
        pwork = carve(2112, F32, "pwork")
        st1 = carve(2112, F32, "st1")
        st2 = carve(2112, F32, "st2")
        rl = carve(2048, F32, "rl")
        assert off[0] <= 32768, off[0]
        uh = vreg[:, :].rearrange("p (f t) -> p f t", t=T)
        vx = vreg[:, :].bitcast(F32).rearrange("p (dc t) -> p dc t", t=T)
        vx_b = [P.buf("vx%d" % dc, "V", 2048 * dc, 2048 * dc + 2048) for dc in range(DC)]
        uh_b = [P.buf("uh%d" % f, "V", 1024 * f, 1024 * f + 1024) for f in range(32)]

        B_x = [P.buf("x%d" % i) for i in range(DC)]
        B_h = [P.buf("h%d" % i) for i in range(DC)]
        B_m2 = [P.buf("m2_%d" % i) for i in range(8)]
        B_y = [P.buf("y%d" % i) for i in range(DC)]
        B_K = [[P.buf("K%d_%d" % (h, j)) for j in range(NJ)] for h in range(4)]
        B_V = [P.buf("V%d" % j) for j in range(NJ)]
        B_zc = [P.buf("zc%d" % c) for c in range(4)]
        B_ca = [P.buf("ca%d" % c) for c in range(4)]
        B_ring = [P.buf("ring%d" % i) for i in range(nr)]
        B_rstd = P.buf("rstd")
        B_rstd2 = P.buf("rstd2")
        B_sq = [P.buf("sq0"), P.buf("sq1")]
        B_spl = P.buf("spl"); B_pw = P.buf("pw"); B_wf = P.buf("wf"); B_cst = P.buf("cst")
        B_G = P.buf("G"); B_bias = P.buf("bias"); B_zt = P.buf("zt"); B_nl = P.buf("nl")
        B_ph = P.buf("ph"); B_sh = P.buf("sh")
        B_ps = [P.buf("psum%d" % i) for i in range(8)]
        B_ya = [P.buf("ya%d" % h) for h in range(4)]
        B_dg = [P.buf("dg%d" % i) for i in range(4)]
        B_id = P.buf("ident")
        B_wbf = [[P.buf("wbf%d_%d" % (l, s)) for s in range(NSLOT)] for l in range(nlayers)]
        B_xs = [[[P.buf("xs%d_%d_%d" % (q, j, dc)) for dc in range(DC)] for j in range(NJ)] for q in range(nseq)]

        tri_f = cstt[:, O_TRI:O_TRI + 128]
        ones_f = cstt[:, O_ONE:O_ONE + 128]
        ymix = [yatt[:, i, :] for i in range(4)] + [hT[:, i, :] for i in range(4, 8)] + [ym2[:, i, :] for i in range(8)]
        B_ymix = B_ya + B_h[4:8] + B_m2

        def act(func, out, in_, reads, writes, **kw):
            return P.op("act", "activation", dict(out=out, in_=in_, func=func, **kw), reads, writes)

        def tt_(eng, out, in0, in1, op, reads, writes):
            return P.op(eng, "tensor_tensor", dict(out=out, in0=in0, in1=in1, op=op), reads, writes)

        def ts_(eng, out, in0, s1, op0, reads, writes, s2=None, op1=None):
            kw = dict(out=out, in0=in0, scalar1=s1, scalar2=s2, op0=op0)
            if op1 is not None:
                kw["op1"] = op1
            return P.op(eng, "tensor_scalar", kw, reads, writes)

        def stt(eng, out, in0, scalar, in1, op0, op1, reads, writes):
            assert eng == "dve"
            return P.op(eng, "scalar_tensor_tensor", dict(out=out, in0=in0, scalar=scalar, in1=in1, op0=op0, op1=op1), reads, writes)

        def mm(out, lhsT, rhs, start, stop, reads, writes, ev):
            return P.op("pe", "matmul", dict(out=out, lhsT=lhsT, rhs=rhs, start=start, stop=stop), reads, writes, ev=ev)

        cidx = [0]
        conv_next = {"l": 1, "s": 0}

        def emit_conversion(l, s):
            P.dma("pool", "c%d" % (cidx[0] % 8), dict(out=wbf[l][s], in_=wpack[l, s]), writes=[B_wbf[l][s]])
            cidx[0] += 1

        def convert_some(n):
            while n > 0 and conv_next["l"] < nlayers:
                emit_conversion(conv_next["l"], conv_next["s"])
                conv_next["s"] += 1
                if conv_next["s"] == NSLOT:
                    conv_next["s"] = 0
                    conv_next["l"] += 1
                n -= 1

        for s in range(NSLOT):
            emit_conversion(0, s)
        gi = [0]

        def gdma(out, in_, reads=(), writes=()):
            k = "g%d" % (gi[0] % 8)
            gi[0] += 1
            return P.dma("sp", k, dict(out=out, in_=in_), reads, writes)

        gdma(cstt[:], cst[:, 0:NCST], writes=[B_cst])
        P.dma("pool", "c7", dict(out=ident_bf[:], in_=cst[:, NCST:NCST + 128]), writes=[B_id])
        P.op("dve", "tensor_copy", dict(out=ones_bf[:], in_=ones_f), [B_cst], [B_cst])
        P.op("dve", "tensor_copy", dict(out=mask_bf[:], in_=tri_f), [B_cst], [B_cst])
        P.op("dve", "memset", dict(ap=epsc[:], constant=EPS), [], [B_cst])
        P.op("dve", "memset", dict(ap=onec[:], constant=1.0), [], [B_cst])

        plan = [(l, s) for l in range(nlayers) for q in range(nseq) for j in range(NJ) for s in range(NSLOT)]
        wq = {"n": 0, "issued": 0}

        def issue_upto(n):
            n = min(n, len(plan))
            while wq["issued"] < n:
                i = wq["issued"]
                l, s = plan[i]
                r = i % nr
                P.dma("sp", "w%d" % r, dict(out=ring[:, r, :], in_=wbf[l][s]), reads=[B_wbf[l][s]], writes=[B_ring[r]])
                wq["issued"] += 1

        def next_slot():
            i = wq["n"]
            wq["n"] += 1
            issue_upto(i + 1)
            r = i % nr
            return ring[:, r, :], B_ring[r], i

        def after_slot(i):
            issue_upto(i + nr + 1)

        issue_upto(nr)
        gen = [0]

        def gbank():
            b = gen[0] % 4
            gen[0] += 1
            return b

        sqi = [0]

        def nsq():
            k = sqi[0] % 2
            sqi[0] += 1
            return k

        def rms_stats(src_tile, src_bufs, r_tile=None, r_buf=None):
            r_tile = rstd if r_tile is None else r_tile
            r_buf = B_rstd if r_buf is None else r_buf
            for dc in range(DC):
                k = nsq()
                act(AF.Square, sqr[:, k, :], src_tile[:, dc, :], [src_bufs[dc]], [B_sq[k]])
                mm(psb[6][:], ones_bf[:], sqr[:, k, :], dc == 0, dc == DC - 1, [B_sq[k], B_cst], [B_ps[6]], True)
            act(AF.Ln, r_tile[:], psb[6][:], [B_ps[6], B_cst], [r_buf], bias=epsc[:], scale=1.0 / D)
            act(AF.Exp, r_tile[:], r_tile[:], [r_buf], [r_buf], scale=-0.5)

        def normalize(goff, src_tile, src_bufs, r_tile, r_buf):
            for dc in range(DC):
                stt("dve", hT[:, dc, :], src_tile[:, dc, :], spl[:, goff + dc:goff + dc + 1], r_tile[:], ALU.mult, ALU.mult,
                    [src_bufs[dc], r_buf, B_spl], [B_h[dc]])

        def prenorm(goff):
            rms_stats(xres, B_x)
            normalize(goff, xres, B_x, rstd, B_rstd)

        ystat_pend = []

        def ystat(dch, src_ap, src_bufs):
            ystat_flush()
            k = nsq()
            act(AF.Square, sqr[:, k, :], src_ap, src_bufs, [B_sq[k]])
            ystat_pend.append((dch, k))

        def ystat_flush():
            while ystat_pend:
                dch, k = ystat_pend.pop(0)
                mm(psb[6][:], ones_bf[:], sqr[:, k, :], dch == 0, dch == DC - 1, [B_sq[k], B_cst], [B_ps[6]], True)

        def rstd_from_stats():
            ystat_flush()
            act(AF.Ln, rstd[:], psb[6][:], [B_ps[6], B_cst], [B_rstd], bias=epsc[:], scale=1.0 / D)
            act(AF.Exp, rstd[:], rstd[:], [B_rstd], [B_rstd], scale=-0.5)

        def pn_mults():
            for dc in range(DC):
                tt_("pool" if dc % 3 == 2 else "dve", ybuf[:, dc, :], ybuf[:, dc, :], rstd[:], ALU.mult, [B_y[dc], B_rstd], [B_y[dc]])

        def pn_stts(goff, dst_tile, dst_bufs):
            for dc in range(DC):
                stt("dve", dst_tile[:, dc, :], ybuf[:, dc, :], spl[:, goff + dc:goff + dc + 1], xres[:, dc, :], ALU.mult, ALU.add,
                    [B_y[dc], B_x[dc], B_spl], [dst_bufs[dc]])

        def postnorm_residual(goff, dst_tile, dst_bufs):
            rstd_from_stats()
            pn_mults()
            pn_stts(goff, dst_tile, dst_bufs)

        def wgroup_mm(slot_ap, slot_buf, i2, rhs_list, rhs_bufs, bank, order=None):
            wv = slot_ap.rearrange("p (k c) -> p k c", c=256)
            n = len(rhs_list)
            order = list(range(n)) if order is None else order
            for i, kc in enumerate(order):
                mm(psb[bank][:], wv[:, kc, i2 * 128:(i2 + 1) * 128], rhs_list[kc], i == 0, i == n - 1,
                   [slot_buf, rhs_bufs[kc]], [B_ps[bank]], i == n - 1)

        hlist = [hT[:, dc, :] for dc in range(DC)]
        pf = {"on": False}

        for l in range(nlayers):
            gdma(spl[:], spk[l], writes=[B_spl])
            P.op("dve", "tensor_copy", dict(out=pw_bf[:], in_=spl[:, O_PW:O_PW + 512]), [B_spl], [B_pw])
            P.op("dve", "tensor_copy", dict(out=wf_bf[:], in_=spl[:, O_WF:O_WF + 64]), [B_spl], [B_wf])
            src = xin if l == 0 else xs
            dst = out if l == nlayers - 1 else xs
            for q in range(nseq):
                P.op("pool", "memset", dict(ap=zconf[:, :, 0:30], constant=0.0), [], B_zc)
                P.op("pool", "memset", dict(ap=phalo[:], constant=0.0), [], [B_ph])
                P.op("pool", "memset", dict(ap=shalo[:], constant=0.0), [], [B_sh])
                P.op("pool", "memset", dict(ap=Gst[:, 0, :], constant=0.0), [], [B_G])
                for j in range(NJ):
                    t0 = j * T
                    sv = src[q].rearrange("(dc p) t -> p dc t", p=128)
                    if not pf["on"]:
                        for dc in range(DC):
                            rd = [] if l == 0 else [B_xs[q][j][dc]]
                            gdma(xres[:, dc, :], sv[:, dc, t0:t0 + T], reads=rd, writes=[B_x[dc]])
                    if conv_next["l"] <= l + 1:
                        convert_some(-(-NSLOT // (nseq * NJ)))
                    if not pf["on"]:
                        prenorm(O_G1)
                    pf["on"] = False
                    for tt in range(4):
                        for dc in range(DC):
                            mm(psb[7][:, tt * 4:tt * 4 + 4], hT[:, dc, tt * 128:(tt + 1) * 128], wf_bf[:, dc * 4:dc * 4 + 4],
                               dc == 0, dc == DC - 1, [B_h[dc], B_wf], [B_ps[7]], dc == DC - 1)
                    tt_("dve", zt[:], psb[7][:, 0:16], spl[:, O_BF:O_BF + 16], ALU.add, [B_ps[7], B_spl], [B_zt])
                    act(AF.Exp, nl[:], zt[:], [B_zt], [B_nl], scale=-1.0)
                    act(AF.Ln, nl[:], nl[:], [B_nl, B_cst], [B_nl], bias=onec[:], scale=1.0)
                    for tt in range(4):
                        mm(psb[7][:, 32 + tt * 4:36 + tt * 4], tri_f, nl[:, tt * 4:tt * 4 + 4], True, True, [B_nl, B_cst], [B_ps[7]], False)
                        mm(psb[7][:, 64 + tt * 4:68 + tt * 4], ones_f, nl[:, tt * 4:tt * 4 + 4], True, True, [B_nl, B_cst], [B_ps[7]], tt == 3)
                    for tt in range(4):
                        qb = 4 * j + tt
                        tt_("dve", Gcol[:, qb, :], psb[7][:, 32 + tt * 4:36 + tt * 4], Gst[:, qb, :], ALU.add, [B_ps[7], B_G], [B_G])
                        tt_("dve", Gst[:, qb + 1, :], psb[7][:, 64 + tt * 4:68 + tt * 4], Gst[:, qb, :], ALU.add, [B_ps[7], B_G], [B_G])
                    for tt in range(4):
                        qb = 4 * j + tt
                        tt_("dve", biasT[:, tt, 0:qb + 1, :], Gcol[:, 0:qb + 1, :], Gst[:, qb:qb + 1, :].broadcast_to([128, qb + 1, 4]),
                            ALU.subtract, [B_G], [B_bias])

                    nkb = 4 * j + 4
                    vslots = []
                    attq = []
                    conv_pending = []
                    st = {"conv_done": 0, "ln": False, "p_done": False, "pti": 0, "dgi": 0}
                    deferred_final = []

                    def emit_conv_pe(c):
                        bank = gbank()
                        o = O_CDW + 31 * c
                        for k in range(31):
                            i = st["dgi"] % 4
                            st["dgi"] += 1
                            ts_("dve", dgr[:, i, :], ident_bf[:], spl[:, o + k:o + k + 1], ALU.mult, [B_id, B_spl], [B_dg[i]])
                            mm(psb[bank][:], dgr[:, i, :], zconf[:, c, k:k + T], k == 0, k == 30, [B_dg[i], B_zc[c]], [B_ps[bank]], True)
                        act(AF.Copy, cacc[:, c, :], psb[bank][:], [B_ps[bank]], [B_ca[c]])
                        P.op("pool", "tensor_copy", dict(out=zconf[:, c, 0:30], in_=zconf[:, c, T:T + 30]), [B_zc[c]], [B_zc[c]])
                        st["conv_done"] += 1

                    ln_stages = []

                    def emit_ln():
                        mean, b_mean = st1[0][:, 0:T], st1[1]
                        var, b_var = st2[0][:, 0:T], st2[1]
                        tmpl, b_tmpl = pwork[0][:, 0:T], pwork[1]

                        def s0():
                            for c in range(4):
                                k = nsq()
                                act(AF.Copy, sqr[:, k, :], cacc[:, c, :], [B_ca[c]], [B_sq[k]])
                                mm(psb[6][:], ones_bf[:], sqr[:, k, :], c == 0, c == 3, [B_sq[k], B_cst], [B_ps[6]], True)
                            for c in range(4):
                                k = nsq()
                                act(AF.Square, sqr[:, k, :], cacc[:, c, :], [B_ca[c]], [B_sq[k]])
                                mm(psb[7][:], ones_bf[:], sqr[:, k, :], c == 0, c == 3, [B_sq[k], B_cst], [B_ps[7]], True)

                        def s1():
                            act(AF.Copy, mean, psb[6][:], [B_ps[6]], [b_mean], scale=1.0 / 512)
                            act(AF.Square, tmpl, psb[6][:], [B_ps[6]], [b_tmpl], scale=1.0 / 512)

                        def s2():
                            stt("dve", var, psb[7][:], 1.0 / 512, tmpl, ALU.mult, ALU.subtract, [B_ps[7], b_tmpl], [b_var])
                            act(AF.Ln, var, var, [b_var, B_cst], [b_var], bias=epsc[:], scale=1.0)
                            act(AF.Exp, var, var, [b_var], [b_var], scale=-0.5)

                        def s3():
                            for c in range(4):
                                tt_("dve", cacc[:, c, :], cacc[:, c, :], mean, ALU.subtract, [B_ca[c], b_mean], [B_ca[c]])
                            for c in range(4):
                                tt_("dve", cacc[:, c, :], cacc[:, c, :], var, ALU.mult, [B_ca[c], b_var], [B_ca[c]])
                            for c in range(4):
                                ts_("dve", cacc[:, c, :], cacc[:, c, :], spl[:, O_LNG + c:O_LNG + c + 1], ALU.mult, [B_ca[c], B_spl], [B_ca[c]],
                                    s2=spl[:, O_LNB + c:O_LNB + c + 1], op1=ALU.add)

                        def sig(cs):
                            def f():
                                for c in cs:
                                    ga, gb_ = gtmp[c % 2]
                                    act(AF.Exp, ga, cacc[:, c, :], [B_ca[c]], [gb_], scale=-1.0)
                                for c in cs:
                                    ga, gb_ = gtmp[c % 2]
                                    act(AF.Ln, ga, ga, [gb_, B_cst], [gb_], bias=onec[:], scale=1.0)
                                for c in cs:
                                    ga, gb_ = gtmp[c % 2]
                                    act(AF.Exp, ga, ga, [gb_], [gb_], scale=-1.0)
                            return f

                        def fin(cs):
                            def f():
                                for c in cs:
                                    ga, gb_ = gtmp[c % 2]
                                    tt_("dve", cacc[:, c, :], cacc[:, c, :], ga, ALU.mult, [B_ca[c], gb_], [B_ca[c]])
                                    deferred_final.append(c)
                            return f

                        def s5():
                            fin((0, 1))()
                            sig((2, 3))()

                        ln_stages.extend([lambda: None, s0, s1, s2, s3, sig((0, 1)), s5, fin((2, 3))])
                        st["ln"] = True

                    def att_A(h, kb):
                        qa, qbuf = qT_l[h]
                        r = kb - 4 * j
                        c0 = max(r, 0) * 128
                        bank = gbank()
                        mm(psb[bank][:, c0:T], Kc[:, h, kb * 128:(kb + 1) * 128], qa[:, c0:T], True, True,
                           [B_K[h][kb // 4], qbuf], [B_ps[bank]], True)
                        pa, pbf = PT[st["pti"] % 3]
                        st["pti"] += 1
                        for tt in range(max(r, 0), 4):
                            act(AF.Exp, pa[:, tt * 128:(tt + 1) * 128], psb[bank][:, tt * 128:(tt + 1) * 128], [B_ps[bank], B_bias], [pbf],
                                bias=biasT[:, tt, kb, h:h + 1], scale=1.0)
                        if r >= 0:
                            tt_("dve", pa[:, r * 128:(r + 1) * 128], pa[:, r * 128:(r + 1) * 128], mask_bf[:], ALU.mult, [pbf, B_cst], [pbf])
                        return (h, kb, c0, pa, pbf)

                    def att_B(h, kb, c0, pa, pbf):
                        mm(psb[4][:, c0:T], Vc[:, kb, h * 128:(h + 1) * 128], pa[:, c0:T], kb == 0, kb == nkb - 1,
                           [B_V[kb // 4], pbf], [B_ps[4]], False)
                        mm(psb[5][:, c0:T], ones_bf[:], pa[:, c0:T], kb == 0, kb == nkb - 1, [pbf, B_cst], [B_ps[5]], True)
                        if kb == nkb - 1:
                            ra, rb = rl
                            act(AF.Ln, ra, psb[5][:], [B_ps[5]], [rb])
                            act(AF.Exp, ra, ra, [rb], [rb], scale=-1.0)
                            tt_("dve", yatt[:, h, :], psb[4][:], ra, ALU.mult, [B_ps[4], B_ps[5], rb], [B_ya[h]])

                    pendB = []

                    def drain_att(n, flush=False):
                        while n > 0 and attq:
                            h, kb = attq.pop(0)
                            a = att_A(h, kb)
                            if pendB:
                                att_B(*pendB.pop(0))
                            pendB.append(a)
                            n -= 1
                        if flush:
                            while pendB:
                                att_B(*pendB.pop(0))

                    nslots = len(WIN_SLOTS)
                    for sidx, (kind, it) in enumerate(WIN_SLOTS):
                        sl, slb, si = next_slot()
                        if kind == "V":
                            vslots.append((sl, slb, si))
                            if len(vslots) == 2:
                                for tt in range(4):
                                    bank = gbank()
                                    for dc in range(DC):
                                        sl2, slb2, _ = vslots[dc // 8]
                                        wv = sl2.rearrange("p (k c) -> p k c", c=512)
                                        mm(psb[bank][:], hT[:, dc, tt * 128:(tt + 1) * 128], wv[:, dc % 8, :], dc == 0, dc == DC - 1,
                                           [B_h[dc], slb2], [B_ps[bank]], dc == DC - 1)
                                    act(AF.Copy, Vc[:, 4 * j + tt, :], psb[bank][:], [B_ps[bank]], [B_V[j]])
                                after_slot(vslots[0][2])
                                after_slot(vslots[1][2])
                            continue
                        for i2, (kd, ci) in enumerate(it):
                            bank = gbank()
                            wgroup_mm(sl, slb, i2, hlist, B_h, bank)
                            pb = psb[bank][:]
                            Bp = B_ps[bank]
                            if kd == "k":
                                act(AF.Copy, Kc[:, ci, t0:t0 + T], pb, [Bp], [B_K[ci][j]])
                            elif kd == "q":
                                act(AF.Copy, qT_l[ci][0], pb, [Bp], [qT_l[ci][1]], scale=float(SCALE))
                                if ci == 3:
                                    attq.extend((h, kb) for h in range(4) for kb in range(nkb))
                            elif kd == "g":
                                ga, gb_ = gtmp[ci % 2]
                                act(AF.Exp, ga, pb, [Bp], [gb_], scale=-1.0)
                                act(AF.Ln, ga, ga, [gb_, B_cst], [gb_], bias=onec[:], scale=1.0)
                                act(AF.Exp, ga, ga, [gb_], [gb_], scale=-1.0)
                            elif kd == "a":
                                ga, gb_ = gtmp[ci % 2]
                                tt_("dve", zconf[:, ci, 30:30 + T], pb, ga, ALU.mult, [Bp, gb_], [B_zc[ci]])
                                conv_pending.append((ci, sidx))
                            elif kd == "p":
                                pa, pbuf = pwork
                                w = (2, 4, 8, 16)[ci]
                                P.op("pool", "tensor_copy", dict(out=pa[:, 0:16], in_=phalo[:, ci, :]), [B_ph], [pbuf])
                                act(AF.Copy, pa[:, 16:16 + T], pb, [Bp], [pbuf])
                                P.op("pool", "tensor_copy", dict(out=phalo[:, ci, :], in_=pa[:, T:T + 16]), [pbuf], [B_ph])
                                cur, curb = pa, pbuf
                                tmps = [st1, st2]
                                sh = 1
                                ti = 0
                                while sh < w:
                                    na, nb = tmps[ti % 2]
                                    ti += 1
                                    tt_("dve", na[:, sh:16 + T], cur[:, sh:16 + T], cur[:, 0:16 + T - sh], ALU.add, [curb], [nb])
                                    cur, curb = na, nb
                                    sh *= 2
                                oa, ob = tmps[ti % 2]
                                pooled = oa.bitcast(BF16)[:, 0:T]
                                stt("dve", pooled, cur[:, 16:16 + T], 1.0 / w, pa[:, 16:16 + T], ALU.mult, ALU.subtract, [curb, pbuf], [ob])
                                if j == 0:
                                    tt_("dve", cur[:, 16:32], cur[:, 16:32], cstt[:, O_INV + 16 * ci:O_INV + 16 * ci + 16], ALU.mult, [curb, B_cst], [curb])
                                    tt_("dve", pooled[:, 0:16], cur[:, 16:32], pa[:, 16:32], ALU.subtract, [curb, pbuf], [ob])
                                mm(psb[7][:], pw_bf[:, ci * 128:(ci + 1) * 128], pooled, True, True, [ob, B_pw], [B_ps[7]], True)
                                act(AF.Copy, ym2[:, ci, :], psb[7][:], [B_ps[7], B_spl], [B_m2[ci]], scale=spl[:, O_PSC + ci:O_PSC + ci + 1])
                                if ci == 3:
                                    st["p_done"] = True
                            elif kd == "C":
                                ca, cb = ctmp[ci % 2]
                                act(AF.Copy, ca, pb, [Bp], [cb])
                            elif kd == "H":
                                ca, cb = ctmp[ci % 2]
                                za, zb = zwork[ci % 2]
                                P.op("pool", "tensor_copy", dict(out=za[:, 0:2], in_=shalo[:, ci, :]), [B_sh], [zb])
                                tt_("dve", za[:, 2:2 + T], pb, ca, ALU.mult, [Bp, cb], [zb])
                                P.op("pool", "tensor_copy", dict(out=shalo[:, ci, :], in_=za[:, T:T + 2]), [zb], [B_sh])
                                va, vb = cv[ci % 2]
                                o = O_SDW + 3 * ci
                                ts_("dve", va, za[:, 0:T], spl[:, o:o + 1], ALU.mult, [zb, B_spl], [vb])
                                for k in (1, 2):
                                    stt("dve", va, za[:, k:k + T], spl[:, o + k:o + k + 1], va, ALU.mult, ALU.add, [zb, vb, B_spl], [vb])
                            elif kd == "B":
                                va, vb = cv[ci % 2]
                                tt_("dve", ym2[:, 4 + ci, :], pb, va, ALU.mult, [Bp, vb], [B_m2[4 + ci]])
                        after_slot(si)
                        while conv_pending and conv_pending[0][1] < sidx:
                            emit_conv_pe(conv_pending.pop(0)[0])
                        if st["conv_done"] == 4 and st["p_done"] and not st["ln"]:
                            emit_ln()
                        if ln_stages:
                            ln_stages.pop(0)()
                        left = nslots - 1 - sidx
                        if attq:
                            drain_att(-(-len(attq) // max(left, 1)) if left > 0 else len(attq))
                    while conv_pending:
                        emit_conv_pe(conv_pending.pop(0)[0])
                    if not st["ln"]:
                        emit_ln()
                    drain_att(10 ** 6, flush=True)
                    while ln_stages:
                        ln_stages.pop(0)()
                    for c in deferred_final:
                        P.op("act", "activation", dict(out=hT[:, 4 + c, :], in_=cacc[:, c, :], func=AF.Copy), [B_ca[c]], [B_h[4 + c]])

                    for g in range(8):
                        sl, slb, si = next_slot()
                        for i2 in range(2):
                            bank = gbank()
                            dch = 2 * g + i2
                            wgroup_mm(sl, slb, i2, ymix, B_ymix, bank, order=list(range(8, 16)) + list(range(4, 8)) + list(range(0, 4)))
                            act(AF.Copy, ybuf[:, dch, :], psb[bank][:], [B_ps[bank]], [B_y[dch]])
                            ystat(dch, psb[bank][:], [B_ps[bank]])
                        after_slot(si)
                    postnorm_residual(O_G2, xres, B_x)
                    prenorm(O_G3)
                    for hf in range(2):
                        for g in range(16):
                            sl, slb, si = next_slot()
                            for i2 in range(2):
                                bank = gbank()
                                fl = 2 * g + i2
                                wgroup_mm(sl, slb, i2, hlist, B_h, bank)
                                act(AF.Relu, uh[:, fl, :], psb[bank][:], [B_ps[bank]], [uh_b[fl]])
                                eng = "dve" if fl % 2 == 0 else "pool"
                                tt_(eng, uh[:, fl, :], uh[:, fl, :], uh[:, fl, :], ALU.mult, [uh_b[fl]], [uh_b[fl]])
                            after_slot(si)
                        for dch in range(16):
                            sl, slb, si = next_slot()
                            bank = gbank()
                            wv = sl.rearrange("p (k c) -> p k c", c=128)
                            for fc in range(32):
                                mm(psb[bank][:], wv[:, fc, :], uh[:, fc, :], fc == 0, fc == 31, [slb, uh_b[fc]], [B_ps[bank]], fc == 31)
                            if hf == 0:
                                act(AF.Copy, ybuf[:, dch, :], psb[bank][:], [B_ps[bank]], [B_y[dch]])
                            else:
                                tt_("dve", ybuf[:, dch, :], psb[bank][:], ybuf[:, dch, :], ALU.add, [B_ps[bank], B_y[dch]], [B_y[dch]])
                                ystat(dch, ybuf[:, dch, :], [B_y[dch]])
                            after_slot(si)
                    dv = dst[q].rearrange("(dc p) t -> p dc t", p=128)
                    nxt = None
                    if j + 1 < NJ:
                        nxt = (q, j + 1)
                    elif q + 1 < nseq:
                        nxt = (q + 1, 0)
                    if nxt is None:
                        postnorm_residual(O_G4, ybuf, B_y)
                    else:
                        nq, nj = nxt
                        nsv = src[nq].rearrange("(dc p) t -> p dc t", p=128)
                        rstd_from_stats()
                        for dc in range(DC):
                            rd = [] if l == 0 else [B_xs[nq][nj][dc]]
                            gdma(vx[:, dc, :], nsv[:, dc, nj * T:nj * T + T], reads=rd, writes=[vx_b[dc]])
                        rms_stats(vx, vx_b, rstd2, B_rstd2)
                        pn_mults()
                        normalize(O_G1, vx, vx_b, rstd2, B_rstd2)
                        pn_stts(O_G4, ybuf, B_y)
                    for dc in range(DC):
                        wr = [] if l == nlayers - 1 else [B_xs[q][j][dc]]
                        gdma(dv[:, dc, t0:t0 + T], ybuf[:, dc, :], reads=[B_y[dc]], writes=wr)
                    if nxt is not None:
                        for dc in range(DC):
                            rd = [] if l == 0 else [B_xs[nq][nj][dc]]
                            gdma(xres[:, dc, :], nsv[:, dc, nj * T:nj * T + T], reads=rd, writes=[B_x[dc]])
                        pf["on"] = True
        P.wait_all("sp", B_y)
        assert not P.engs["pe"].pending
        assert wq["n"] == len(plan) and wq["issued"] == len(plan), (wq, len(plan))

        with nc.Block() as block:
            @block.sync
            def _(e):
                P.replay("sp", e, sems)

            @block.tensor
            def _(e):
                P.replay("pe", e, sems)

            @block.scalar
            def _(e):
                P.replay("act", e, sems)

            @block.vector
            def _(e):
                P.replay("dve", e, sems)

            @block.gpsimd
            def _(e):
                P.replay("pool", e, sems)
    stats = {k: (E.ninstr, E.count) for k, E in P.engs.items()}
    return nc, stats


N_CORES = 8
LAYERS_PER_LAUNCH = 4

_cache = {}


def _get_nc(nseq, S, nl):
    key = (nseq, S, nl)
    if key not in _cache:
        _cache[key] = build_nc(nseq, S, nl)[0]
    return _cache[key]


def run_layers(xT_shards, params, layers, S):
    nl = len(layers)
    nseq = xT_shards[0].shape[0]
    nc = _get_nc(nseq, S, nl)
    wp = np.stack([pack_weights_layer(params["w_in"][l], params["w_out"][l], params["w_mlp1"][l], params["w_mlp2"][l]) for l in layers])
    sp = np.stack([pack_small_layer(l, params) for l in layers])
    cs = make_consts()
    in_maps = [{"xin": np.ascontiguousarray(xs), "wpack": wp, "spk": sp, "cst": cs} for xs in xT_shards]
    res = run_bass_kernel_spmd(nc, in_maps, core_ids=list(range(len(xT_shards))))
    return [np.asarray(r["out"]) for r in res.results]


def kernel(**inputs):
    p = {k: np.asarray(v) for k, v in inputs.items()}
    x = p["x"]
    B, S, _ = x.shape
    depth = p["w_in"].shape[0]
    xT = np.ascontiguousarray(x.transpose(0, 2, 1))
    per = B // N_CORES
    shards = [xT[c * per:(c + 1) * per] for c in range(N_CORES)]
    for l0 in range(0, depth, LAYERS_PER_LAUNCH):
        shards = run_layers(shards, p, list(range(l0, min(depth, l0 + LAYERS_PER_LAUNCH))), S)
    outT = np.concatenate(shards, axis=0)
    return np.ascontiguousarray(outT.transpose(0, 2, 1)).astype(np.float32)
```
